# Optimizing a Trainium2 kernel written in Bass

```python
import math
import jax, jax.numpy as jnp
from jax import lax
import numpy as np

D_MODEL = 2048
BATCH = 1
SEQ = 8192
DEPTH = 1

HEAD_DIM = 64
A_Q_HEADS = 16
A_KV_HEADS = 4
A_GROUP = A_Q_HEADS // A_KV_HEADS
WINDOW = 128
B_HEADS = 16
BLOCK = 128
D_FF = ((8 * D_MODEL // 3 + 255) // 256) * 256
EPS = 1e-6

A_Q_W = A_Q_HEADS * HEAD_DIM
A_KV_W = A_KV_HEADS * HEAD_DIM
B_W = B_HEADS * HEAD_DIM
IN_SPLITS = (A_Q_W, A_KV_W, A_KV_W, B_W, B_W, B_W, D_MODEL, D_MODEL)
IN_WIDTH = sum(IN_SPLITS)
IN_OFFSETS = tuple(int(o) for o in np.cumsum(IN_SPLITS)[:-1])

kernel_name = "hybrid_swa_sink_stickbreaking_gated"


def rmsnorm(x, g):
    xf = x.astype(jnp.float32)
    inv = lax.rsqrt(jnp.mean(xf * xf, axis=-1, keepdims=True) + EPS)
    return (xf * inv * g.astype(jnp.float32)).astype(x.dtype)


def alibi_slopes(n_heads):
    h = jnp.arange(1, n_heads + 1, dtype=jnp.float32)
    return jnp.exp2(-8.0 * h / n_heads)


def sliding_window_attention(q, k, v, sinks):
    B, S = q.shape[0], q.shape[1]
    nb = S // BLOCK
    qb = q.reshape(B, nb, BLOCK, A_KV_HEADS, A_GROUP, HEAD_DIM)
    kb = k.reshape(B, nb, BLOCK, A_KV_HEADS, HEAD_DIM)
    vb = v.reshape(B, nb, BLOCK, A_KV_HEADS, HEAD_DIM)
    pad_k = jnp.zeros_like(kb[:, :1])
    pad_v = jnp.zeros_like(vb[:, :1])
    kk = jnp.concatenate([jnp.concatenate([pad_k, kb[:, :-1]], axis=1), kb], axis=2)
    vv = jnp.concatenate([jnp.concatenate([pad_v, vb[:, :-1]], axis=1), vb], axis=2)

    scale = 1.0 / math.sqrt(HEAD_DIM)
    s = jnp.einsum('bnqhgd,bnkhd->bnhgqk', qb, kk).astype(jnp.float32) * scale

    qi = jnp.arange(BLOCK)[:, None]
    ki = jnp.arange(2 * BLOCK)[None, :]
    dist = (BLOCK + qi - ki)
    blk = jnp.arange(nb)[:, None, None]
    valid = (dist >= 0) & (dist < WINDOW)
    valid = valid[None] & ((blk > 0) | (ki[None] >= BLOCK))

    slopes = alibi_slopes(A_Q_HEADS).reshape(A_KV_HEADS, A_GROUP)
    s = s - slopes[:, :, None, None] * dist.astype(jnp.float32)[None, None]
    s = jnp.where(valid[None, :, None, None], s, -jnp.inf)

    sink = sinks.astype(jnp.float32).reshape(A_KV_HEADS, A_GROUP)[None, None, :, :, None, None]
    sink = jnp.broadcast_to(sink, s.shape[:-1] + (1,))
    p = jax.nn.softmax(jnp.concatenate([s, sink], axis=-1), axis=-1)[..., :-1]
    o = jnp.einsum('bnhgqk,bnkhd->bnqhgd', p.astype(v.dtype), vv)
    return o.reshape(B, S, A_Q_W)


def stick_breaking_attention(q, k, v):
    B, S = q.shape[0], q.shape[1]
    nb = S // BLOCK
    scale = 1.0 / math.sqrt(HEAD_DIM)
    qb = q.reshape(B, nb, BLOCK, B_HEADS, HEAD_DIM).transpose(1, 0, 2, 3, 4)
    kpos = jnp.arange(S)

    def one_block(args):
        qblk, i = args
        z = jnp.einsum('bqhd,bkhd->bhqk', qblk, k).astype(jnp.float32) * scale
        tpos = i * BLOCK + jnp.arange(BLOCK)
        causal = kpos[None, :] < tpos[:, None]
        log_beta = jax.nn.log_sigmoid(z)
        log_fail = jnp.where(causal, jax.nn.log_sigmoid(-z), 0.0)
        suffix = lax.cumsum(log_fail, axis=3, reverse=True) - log_fail
        a = jnp.where(causal, jnp.exp(log_beta + suffix), 0.0)
        return jnp.einsum('bhqk,bkhd->bqhd', a.astype(v.dtype), v)

    o = lax.map(one_block, (qb, jnp.arange(nb)))
    return o.transpose(1, 0, 2, 3, 4).reshape(B, S, B_W)


def swiglu(h, w_in, w_down):
    gu = h @ w_in
    gate, up = jnp.split(gu, 2, axis=-1)
    return (jax.nn.silu(gate) * up) @ w_down


def setup_inputs(seed: int = 0) -> dict:
    key = jax.random.key(seed)
    ks = jax.random.split(key, 12)
    f32 = jnp.float32

    def w(k, shape, fan_in):
        return jax.random.normal(k, shape, f32) * (fan_in ** -0.5)

    def gain(k, shape):
        return 1.0 + 0.02 * jax.random.normal(k, shape, f32)

    return {
        "x": jax.random.normal(ks[0], (BATCH, SEQ, D_MODEL), f32),
        "norm_mix_g": gain(ks[1], (DEPTH, D_MODEL)),
        "w_in": w(ks[2], (DEPTH, D_MODEL, IN_WIDTH), D_MODEL),
        "sink_logits": 0.5 * jax.random.normal(ks[3], (DEPTH, A_Q_HEADS), f32),
        "w_branch_a": w(ks[4], (DEPTH, A_Q_W, D_MODEL), A_Q_W),
        "w_branch_b": w(ks[5], (DEPTH, B_W, D_MODEL), B_W),
        "w_out": w(ks[6], (DEPTH, D_MODEL, D_MODEL), D_MODEL),
        "norm_ffn_g": gain(ks[7], (DEPTH, D_MODEL)),
        "w_ffn_in": w(ks[8], (DEPTH, D_MODEL, 2 * D_FF), D_MODEL),
        "w_ffn_down": w(ks[9], (DEPTH, D_FF, D_MODEL), D_FF),
        "norm_final_g": gain(ks[10], (D_MODEL,)),
    }


def reference(x, norm_mix_g, w_in, sink_logits, w_branch_a, w_branch_b, w_out,
              norm_ffn_g, w_ffn_in, w_ffn_down, norm_final_g):
    B, S, _ = x.shape
    for layer in range(DEPTH):
        h = rmsnorm(x, norm_mix_g[layer])
        proj = h @ w_in[layer]
        qa, ka, va, qb, kb, vb, ga, gb = jnp.split(proj, IN_OFFSETS, axis=-1)

        ya = sliding_window_attention(
            qa.reshape(B, S, A_Q_HEADS, HEAD_DIM),
            ka.reshape(B, S, A_KV_HEADS, HEAD_DIM),
            va.reshape(B, S, A_KV_HEADS, HEAD_DIM),
            sink_logits[layer]) @ w_branch_a[layer]
        yb = stick_breaking_attention(
            qb.reshape(B, S, B_HEADS, HEAD_DIM),
            kb.reshape(B, S, B_HEADS, HEAD_DIM),
            vb.reshape(B, S, B_HEADS, HEAD_DIM)) @ w_branch_b[layer]

        merged = jax.nn.sigmoid(ga) * ya + jax.nn.sigmoid(gb) * yb
        x = x + merged @ w_out[layer]
        x = x + swiglu(rmsnorm(x, norm_ffn_g[layer]), w_ffn_in[layer], w_ffn_down[layer])
    return rmsnorm(x, norm_final_g)
```

```python
import contextlib
import math

import numpy as np
import ml_dtypes

import concourse.bass as bass
import concourse.mybir as mybir
from concourse.bass_utils import run_bass_kernel_spmd

F32 = mybir.dt.float32
BF16 = mybir.dt.bfloat16
I32 = mybir.dt.int32
AF = mybir.ActivationFunctionType
ALU = mybir.AluOpType
AX = mybir.AxisListType

PE, ACT, DVE, POOL, SP = "pe", "act", "dve", "pool", "sp"
ENGS = (PE, ACT, DVE, POOL, SP)

D = 2048
S = 8192
NCORE = 8
TOK = S // NCORE
KC = D // 128
DFF = 5632
EPS = 1e-6
W1C = 704
NEG_BIG = -30000.0
STOP_AFTER = None
SWA_T = True
DBG_1A_TILES = None
DBG_1B_BLOCKS = None
DBG_1C_GROUPS = None


class Buf:
    def __init__(self, name, ap=None):
        self.name = name
        self.ap = ap
        self.w = {}
        self.r = {}
        self.gen = None


class Rec:
    _gen = 0

    def __init__(self):
        Rec._gen += 1
        self.gen = Rec._gen
        self.ops = {e: [] for e in ENGS}
        self.cnt = {}
        self.waited = {e: {} for e in ENGS}
        self.pending = {e: {} for e in ENGS}
        self.nstream = 0
        self.needed = {}
        self.dma_keys = set()

    def new_stream(self):
        self.nstream += 1
        return "d%d" % self.nstream

    def barrier(self):
        snap = dict(self.cnt)
        for e in ENGS:
            for k, v in snap.items():
                if self.pending[e].get(k, 0) < v:
                    self.pending[e][k] = v

    def emit(self, eng, fn, reads=(), writes=(), stream=None, inc=None):
        key = stream if stream is not None else eng
        is_dma = stream is not None
        waits = {}

        def need(k, v):
            if k == PE and eng == PE:
                return
            if waits.get(k, 0) < v:
                waits[k] = v

        for b in list(reads) + list(writes):
            if b.gen != self.gen:
                b.gen = self.gen
                b.w = {}
                b.r = {}
        for b in reads:
            for k, v in b.w.items():
                need(k, v)
        for b in writes:
            for k, v in b.w.items():
                need(k, v)
            for k, v in b.r.items():
                need(k, v)
        for k, v in self.pending[eng].items():
            need(k, v)
        self.pending[eng] = {}
        wl = []
        for k, v in waits.items():
            if self.waited[eng].get(k, 0) < v:
                self.waited[eng][k] = v
                wl.append((k, v))
                self.needed.setdefault(k, set()).add(v)
        self.cnt[key] = self.cnt.get(key, 0) + 1
        val = self.cnt[key]
        if is_dma:
            if inc != 1:
                self.dma_keys.add(key)
            self.needed.setdefault(key, set()).add(val)
        self.ops[eng].append((wl, fn, key, val))
        for b in writes:
            b.w = {key: val}
            b.r = {}
        for b in reads:
            if b not in writes:
                b.r[key] = val
        return (key, val)

    def replay(self, nc, semstack, prefix):
        keys = sorted(k for k in self.cnt.keys() if self.needed.get(k))
        rank = {}
        for k in keys:
            for i, v in enumerate(sorted(self.needed[k])):
                rank[(k, v)] = i + 1
        mult = {k: (16 if k in self.dma_keys else 1) for k in keys}
        final_waits = [(k, self.cnt[k]) for k in keys if k in self.dma_keys]
        sems = {k: semstack.enter_context(nc.semaphore("s%s_%s" % (prefix, k))) for k in keys}
        with contextlib.ExitStack() as st:
            block = st.enter_context(nc.Block())

            def run(eng_name):
                def f(e):
                    for wl, fn, key, val in self.ops[eng_name]:
                        for k, v in wl:
                            e.wait_ge(sems[k], rank[(k, v)] * mult[k])
                        ins = fn(e)
                        if (key, val) in rank:
                            ins.then_inc(sems[key], mult[key])
                    if eng_name == SP:
                        for k, v in final_waits:
                            e.wait_ge(sems[k], rank[(k, v)] * mult[k])
                return f

            block.tensor(run(PE))
            block.scalar(run(ACT))
            block.vector(run(DVE))
            block.gpsimd(run(POOL))
            block.sync(run(SP))


def mm_group(e, out_ap, pairs, start=True, stop=True, skip=False):
    n = len(pairs)
    last = None
    for i, (l, r) in enumerate(pairs):
        if skip:
            last = e.matmul(out_ap, lhsT=l, rhs=r, start=(start and i == 0), stop=(stop and i == n - 1),
                            skip_group_check=True)
        else:
            last = e.matmul(out_ap, lhsT=l, rhs=r, start=(start and i == 0), stop=(stop and i == n - 1))
    return last


def build_program():
    nc = bass.Bass("TRN2", target_bir_lowering=False)

    def din(name, shape, dt):
        return nc.dram_tensor(name, shape, dt, kind="ExternalInput").ap()

    xT = din("xT", [D, S], F32)
    w1 = din("w1", [D, W1C], F32)
    gvec = din("gvec", [128, 3 * KC], F32)
    swab_d = din("swab", [128, 1024], F32)
    sink_d = din("sinkt", [128, 2], F32)
    gidx_d = din("gidx", [128, 16], I32)
    cmat_d = din("cmat", [128, 4 * 128], BF16)
    mask_d = din("masks", [128, 4 * 1024], BF16)
    identf_d = din("identf", [128, 128], F32)
    outT = nc.dram_tensor("outT", [D, TOK], F32, kind="ExternalOutput").ap()
    gb_in = nc.dram_tensor("gb_in", [256, S], BF16)
    gb_out = nc.dram_tensor("gb_out", [NCORE * 256, S], BF16)

    class _Cx:
        pass
    cx = _Cx()
    cx.R = Rec()
    xT_v = xT.rearrange("(k p) n -> p k n", p=128)

    with contextlib.ExitStack() as top:
        def sbuf(st, name, shape, dt):
            return Buf(name, st.enter_context(nc.sbuf_tensor(name, shape, dt)))

        def psum(st, name, shape, dt=F32):
            return Buf(name, st.enter_context(nc.psum_tensor(name, shape, dt)))

        def load(eng, buf, dst_ap, src_ap, stream):
            return cx.R.emit(eng, lambda e: e.dma_start(out=dst_ap, in_=src_ap), writes=[buf], stream=stream)

        CM = sbuf(top, "CM", [128, 512], BF16)
        MASK = sbuf(top, "MASK", [128, 4096], BF16)
        GV = sbuf(top, "GV", [128, 3 * KC], F32)
        SWAB = sbuf(top, "SWAB", [128, 1024], F32)
        SINK = sbuf(top, "SINK", [128, 2], F32)
        GIDX = sbuf(top, "GIDX", [128, 16], I32)
        IDF = sbuf(top, "IDF", [128, 128], F32)
        for b, src in ((CM, cmat_d), (MASK, mask_d), (GV, gvec), (SWAB, swab_d), (SINK, sink_d), (GIDX, gidx_d), (IDF, identf_d)):
            load(SP, b, b.ap[:], src, cx.R.new_stream())
        ONES = CM.ap[:, 0:128]
        IDENT = CM.ap[:, 128:256]
        TRI = CM.ap[:, 256:384]
        TRI2 = CM.ap[:, 384:512]

        def norm_tile(src_buf, src_ap, gcol, SQ, SSP, RB, dst_buf, dst_ap_k, TT):
            cx.R.emit(ACT, lambda e: e.activation(out=SQ.ap[:, :, 0:TT], in_=src_ap, func=AF.Square),
                   reads=[src_buf], writes=[SQ])
            cx.R.emit(PE, lambda e: mm_group(e, SSP.ap[:, 0:TT], [(ONES, SQ.ap[:, k, 0:TT]) for k in range(KC)]),
                   reads=[SQ, CM], writes=[SSP])
            cx.R.emit(DVE, lambda e: e.tensor_scalar(out=RB.ap[:, 0:TT], in0=SSP.ap[:, 0:TT], scalar1=1.0 / D, scalar2=EPS,
                                                  op0=ALU.mult, op1=ALU.add), reads=[SSP], writes=[RB])
            cx.R.emit(ACT, lambda e: e.activation(out=RB.ap[:, 0:TT], in_=RB.ap[:, 0:TT], func=AF.Sqrt), reads=[RB], writes=[RB])
            cx.R.emit(DVE, lambda e: e.reciprocal(out=RB.ap[:, 0:TT], in_=RB.ap[:, 0:TT]), reads=[RB], writes=[RB])

            def f(e):
                last = None
                for k in range(KC):
                    last = e.scalar_tensor_tensor(out=dst_ap_k(k), in0=src_ap[:, k, :], scalar=GV.ap[:, gcol + k:gcol + k + 1],
                                                  in1=RB.ap[:, 0:TT], op0=ALU.mult, op1=ALU.mult)
                return last
            cx.R.emit(DVE, f, reads=[src_buf, RB, GV], writes=[dst_buf])

        with contextlib.ExitStack() as p1:
            QBT = sbuf(p1, "QBT", [128, S], BF16)
            KBT = [sbuf(p1, "KBT%d" % h, [128, S], BF16) for h in range(2)]
            QAT = sbuf(p1, "QAT", [128, S], BF16)
            KAT = [sbuf(p1, "KAT%d" % h, [128, S], BF16) for h in range(2)]
            for h in range(2):
                z0, z1 = (64, 128) if h == 0 else (0, 64)
                cx.R.emit(DVE, lambda e, h=h, z0=z0, z1=z1: e.memset(KBT[h].ap[z0:z1, :], 0.0), writes=[KBT[h]])
                cx.R.emit(DVE, lambda e, h=h, z0=z0, z1=z1: e.memset(KAT[h].ap[z0:z1, :], 0.0), writes=[KAT[h]])
            VALL = sbuf(p1, "VALL", [128, 64, 192], BF16)
            GBIN_A = [Buf("gbinA%d" % i) for i in range(16)]
            GBIN_B = [Buf("gbinB%d" % i) for i in range(16)]

            with contextlib.ExitStack() as pa:
                TT = 256
                W1B = sbuf(pa, "W1B", [128, KC, W1C], BF16)
                XS = [sbuf(pa, "XS%d" % i, [128, KC, TT], F32) for i in range(1)]
                sXS = [cx.R.new_stream() for _ in range(1)]
                SQ = sbuf(pa, "SQ", [128, KC, TT], BF16)
                HN = [sbuf(pa, "HN%d" % i, [128, KC, TT], BF16) for i in range(2)]
                RB = [sbuf(pa, "RB%d" % i, [128, TT], F32) for i in range(2)]
                SSP = psum(pa, "SSP", [128, 512])
                PP = [psum(pa, "PP%d" % i, [128, 512]) for i in range(3)]
                VP = [psum(pa, "VP%d" % i, [128, 512]) for i in range(2)]
                load(POOL, W1B, W1B.ap[:], w1.rearrange("(k p) n -> p k n", p=128), cx.R.new_stream())
                dsts = [(QBT, 0, ACT), (KBT, 128, DVE), (QAT, 256, ACT), (KAT, 384, DVE)]
                pp_i = 0
                vp_i = 0
                for t in range(S // TT if DBG_1A_TILES is None else DBG_1A_TILES):
                    sl = t % 2
                    t0 = t * TT
                    load(SP, XS[0], XS[0].ap[:], xT_v[:, :, t0:t0 + TT], sXS[0])
                    norm_tile(XS[0], XS[0].ap[:], 0, SQ, SSP, RB[sl], HN[sl], lambda k, sl=sl: HN[sl].ap[:, k, :], TT)
                    for (dst, c0, evac) in dsts:
                        P = PP[pp_i % 3]
                        pp_i += 1
                        cx.R.emit(PE, lambda e, P=P, c0=c0, sl=sl: mm_group(
                            e, P.ap[:, 0:TT], [(W1B.ap[:, k, c0:c0 + 128], HN[sl].ap[:, k, :]) for k in range(KC)]),
                            reads=[W1B, HN[sl]], writes=[P])
                        if isinstance(dst, list):
                            def f_split(e, P=P, dst=dst, t0=t0):
                                e.tensor_copy(out=dst[0].ap[0:64, t0:t0 + TT], in_=P.ap[0:64, 0:TT])
                                return e.tensor_copy(out=dst[1].ap[64:128, t0:t0 + TT], in_=P.ap[64:128, 0:TT])
                            cx.R.emit(DVE, f_split, reads=[P], writes=[dst[0], dst[1]])
                        elif evac == ACT:
                            cx.R.emit(ACT, lambda e, P=P, dst=dst, t0=t0: e.activation(out=dst.ap[:, t0:t0 + TT], in_=P.ap[:, 0:TT], func=AF.Copy),
                                   reads=[P], writes=[dst])
                        else:
                            cx.R.emit(DVE, lambda e, P=P, dst=dst, t0=t0: e.tensor_copy(out=dst.ap[:, t0:t0 + TT], in_=P.ap[:, 0:TT]),
                                   reads=[P], writes=[dst])
                    for bi in range(TT // 128):
                        V = VP[vp_i % 2]
                        vp_i += 1
                        blk = t * (TT // 128) + bi
                        cx.R.emit(PE, lambda e, V=V, bi=bi, sl=sl: mm_group(
                            e, V.ap[:, 0:192], [(HN[sl].ap[:, k, bi * 128:(bi + 1) * 128], W1B.ap[:, k, 512:704]) for k in range(KC)]),
                            reads=[W1B, HN[sl]], writes=[V])
                        cx.R.emit(DVE, lambda e, V=V, blk=blk: e.tensor_copy(out=VALL.ap[:, blk, :], in_=V.ap[:, 0:192]),
                               reads=[V], writes=[VALL])
                cx.R.barrier()
                if STOP_AFTER == "a":
                    return nc

            with contextlib.ExitStack() as pb:
              if SWA_T:
                ST = [psum(pb, "ST%d" % i, [128, 1024]) for i in range(2)]
                DENP = [psum(pb, "DENP%d" % i, [128, 512]) for i in range(2)]
                OP = [psum(pb, "OPt%d" % i, [128, 512]) for i in range(2)]
                SBT = [sbuf(pb, "SBT%d" % i, [128, 512], F32) for i in range(2)]
                PTT = [sbuf(pb, "PTT%d" % i, [128, 512], BF16) for i in range(2)]
                DEN = [sbuf(pb, "DEN%d" % i, [128, 128], F32) for i in range(2)]
                ESINK = sbuf(pb, "ESINK", [128, 2], F32)
                OAS = [sbuf(pb, "OASt%d" % i, [128, 512], BF16) for i in range(2)]
                sOAS = [cx.R.new_stream() for _ in range(2)]
                cx.R.emit(ACT, lambda e: e.activation(out=ESINK.ap[:], in_=SINK.ap[:], func=AF.Exp), reads=[SINK], writes=[ESINK])
                for i in range(64 if DBG_1B_BLOCKS is None else DBG_1B_BLOCKS):
                    sl = i % 2
                    ip = max(i - 1, 0)
                    boff = 0 if i > 0 else 512

                    def f_scores(e, i=i, ip=ip, sl=sl):
                        last = None
                        for h in range(2):
                            for kh, kb in ((0, ip), (1, i)):
                                last = e.matmul(ST[sl].ap[:, h * 512 + kh * 128: h * 512 + (kh + 1) * 128],
                                                lhsT=KAT[h].ap[:, kb * 128:(kb + 1) * 128],
                                                rhs=QAT.ap[:, i * 128:(i + 1) * 128], start=True, stop=True)
                        return last
                    cx.R.emit(PE, f_scores, reads=[QAT, KAT[0], KAT[1]], writes=[ST[sl]])
                    cx.R.emit(DVE, lambda e, sl=sl, boff=boff: e.scalar_tensor_tensor(
                        out=SBT[sl].ap[:].rearrange("p (h k) -> p h k", h=2),
                        in0=ST[sl].ap[:].rearrange("p (h k) -> p h k", h=2)[:, :, 0:256], scalar=0.125,
                        in1=SWAB.ap[:, boff:boff + 512].rearrange("p (h k) -> p h k", h=2), op0=ALU.mult, op1=ALU.add),
                        reads=[ST[sl], SWAB], writes=[SBT[sl]])
                    cx.R.emit(ACT, lambda e, sl=sl: e.activation(out=PTT[sl].ap[:], in_=SBT[sl].ap[:], func=AF.Exp),
                              reads=[SBT[sl]], writes=[PTT[sl]])

                    def f_den(e, sl=sl):
                        last = None
                        for h in range(2):
                            for kh in range(2):
                                last = e.matmul(DENP[sl].ap[h * 64:(h + 1) * 64, 0:128], lhsT=ONES[:, 0:64],
                                                rhs=PTT[sl].ap[:, (h * 2 + kh) * 128:(h * 2 + kh + 1) * 128],
                                                start=(kh == 0), stop=(kh == 1))
                        return last
                    cx.R.emit(PE, f_den, reads=[PTT[sl], CM], writes=[DENP[sl]])

                    def f_av(e, i=i, ip=ip, sl=sl):
                        last = None
                        for h in range(2):
                            for kh, kb in ((0, ip), (1, i)):
                                last = e.matmul(OP[sl].ap[h * 64:(h + 1) * 64, 0:128], lhsT=VALL.ap[:, kb, 128:192],
                                                rhs=PTT[sl].ap[:, (h * 2 + kh) * 128:(h * 2 + kh + 1) * 128],
                                                start=(kh == 0), stop=(kh == 1))
                        return last
                    cx.R.emit(PE, f_av, reads=[VALL, PTT[sl]], writes=[OP[sl]])

                    def f_dadd(e, sl=sl):
                        last = None
                        for h in range(2):
                            last = e.tensor_scalar(out=DEN[sl].ap[h * 64:(h + 1) * 64, :], in0=DENP[sl].ap[h * 64:(h + 1) * 64, 0:128],
                                                   scalar1=ESINK.ap[h * 64:(h + 1) * 64, h:h + 1], scalar2=None, op0=ALU.add)
                        return last
                    cx.R.emit(DVE, f_dadd, reads=[DENP[sl], ESINK], writes=[DEN[sl]])
                    cx.R.emit(DVE, lambda e, sl=sl: e.reciprocal(out=DEN[sl].ap[:], in_=DEN[sl].ap[:]), reads=[DEN[sl]], writes=[DEN[sl]])
                    g4 = i // 4
                    osl = g4 % 2
                    cx.R.emit(DVE, lambda e, sl=sl, osl=osl, i=i: e.tensor_tensor(
                        out=OAS[osl].ap[:, (i % 4) * 128:(i % 4 + 1) * 128], in0=OP[sl].ap[:, 0:128], in1=DEN[sl].ap[:], op=ALU.mult),
                        reads=[OP[sl], DEN[sl]], writes=[OAS[osl]])
                    if i % 4 == 3:
                        cx.R.emit(SP, lambda e, osl=osl, g4=g4: e.dma_start(out=gb_in.ap()[0:128, g4 * 512:(g4 + 1) * 512], in_=OAS[osl].ap[:]),
                                  reads=[OAS[osl]], writes=[GBIN_A[g4]], stream=sOAS[osl])
                cx.R.barrier()
                if STOP_AFTER == "b":
                    return nc
            with contextlib.ExitStack() as pb:
              if not SWA_T:
                  SA = [psum(pb, "SA%d" % i, [128, 1024]) for i in range(2)]
                  PT = [psum(pb, "PT%d" % i, [128, 1024], BF16) for i in range(2)]
                  OP = [psum(pb, "OP%d" % i, [128, 512]) for i in range(2)]
                  SB = [sbuf(pb, "SB%d" % i, [128, 512], F32) for i in range(2)]
                  PX = [sbuf(pb, "PX%d" % i, [128, 512], F32) for i in range(2)]
                  PN = [sbuf(pb, "PN%d" % i, [128, 512], BF16) for i in range(2)]
                  PTS = [sbuf(pb, "PTS%d" % i, [128, 512], BF16) for i in range(2)]
                  MX = [sbuf(pb, "MX%d" % i, [128, 2], F32) for i in range(2)]
                  NM = [sbuf(pb, "NM%d" % i, [128, 2], F32) for i in range(2)]
                  SD = [sbuf(pb, "SD%d" % i, [128, 2], F32) for i in range(2)]
                  ES = [sbuf(pb, "ES%d" % i, [128, 2], F32) for i in range(2)]
                  SS = [sbuf(pb, "SS%d" % i, [128, 2], F32) for i in range(2)]
                  OAS = [sbuf(pb, "OAS%d" % i, [128, 512], BF16) for i in range(2)]
                  sOAS = [cx.R.new_stream() for _ in range(2)]
                  for i in range(64 if DBG_1B_BLOCKS is None else DBG_1B_BLOCKS):
                      sl = i % 2
                      ip = max(i - 1, 0)
                      boff = 0 if i > 0 else 512

                      def f_scores(e, i=i, ip=ip, sl=sl):
                          last = None
                          for h in range(2):
                              for kh, kb in ((0, ip), (1, i)):
                                  last = e.matmul(SA[sl].ap[:, h * 512 + kh * 128: h * 512 + (kh + 1) * 128],
                                                  lhsT=QAT.ap[h * 64:(h + 1) * 64, i * 128:(i + 1) * 128],
                                                  rhs=KAT.ap[h * 64:(h + 1) * 64, kb * 128:(kb + 1) * 128], start=True, stop=True)
                          return last
                      cx.R.emit(PE, f_scores, reads=[QAT, KAT], writes=[SA[sl]])
                      cx.R.emit(DVE, lambda e, sl=sl, boff=boff: e.scalar_tensor_tensor(
                          out=SB[sl].ap[:].rearrange("p (h k) -> p h k", h=2),
                          in0=SA[sl].ap[:].rearrange("p (h k) -> p h k", h=2)[:, :, 0:256], scalar=0.125,
                          in1=SWAB.ap[:, boff:boff + 512].rearrange("p (h k) -> p h k", h=2), op0=ALU.mult, op1=ALU.add),
                          reads=[SA[sl], SWAB], writes=[SB[sl]])
                      cx.R.emit(DVE, lambda e, sl=sl: e.tensor_reduce(out=MX[sl].ap[:], in_=SB[sl].ap[:].rearrange("p (h k) -> p h k", h=2),
                                                                   axis=AX.X, op=ALU.max), reads=[SB[sl]], writes=[MX[sl]])
                      cx.R.emit(DVE, lambda e, sl=sl: e.tensor_tensor(out=MX[sl].ap[:], in0=MX[sl].ap[:], in1=SINK.ap[:], op=ALU.max),
                             reads=[MX[sl], SINK], writes=[MX[sl]])
                      cx.R.emit(DVE, lambda e, sl=sl: e.tensor_scalar(out=NM[sl].ap[:], in0=MX[sl].ap[:], scalar1=-1.0, scalar2=None, op0=ALU.mult),
                             reads=[MX[sl]], writes=[NM[sl]])
                      cx.R.emit(DVE, lambda e, sl=sl: e.tensor_tensor(out=SD[sl].ap[:], in0=SINK.ap[:], in1=NM[sl].ap[:], op=ALU.add),
                             reads=[NM[sl], SINK], writes=[SD[sl]])

                      def f_exp(e, sl=sl):
                          last = None
                          for h in range(2):
                              last = e.activation(out=PX[sl].ap[:, h * 256:(h + 1) * 256], in_=SB[sl].ap[:, h * 256:(h + 1) * 256],
                                                  func=AF.Exp, bias=NM[sl].ap[:, h:h + 1], scale=1.0, accum_out=SS[sl].ap[:, h:h + 1])
                          return last
                      cx.R.emit(ACT, f_exp, reads=[SB[sl], NM[sl]], writes=[PX[sl], SS[sl]])
                      cx.R.emit(ACT, lambda e, sl=sl: e.activation(out=ES[sl].ap[:], in_=SD[sl].ap[:], func=AF.Exp), reads=[SD[sl]], writes=[ES[sl]])
                      cx.R.emit(DVE, lambda e, sl=sl: e.tensor_tensor(out=SS[sl].ap[:], in0=SS[sl].ap[:], in1=ES[sl].ap[:], op=ALU.add),
                             reads=[SS[sl], ES[sl]], writes=[SS[sl]])
                      cx.R.emit(DVE, lambda e, sl=sl: e.reciprocal(out=SS[sl].ap[:], in_=SS[sl].ap[:]), reads=[SS[sl]], writes=[SS[sl]])

                      def f_pn(e, sl=sl):
                          last = None
                          for h in range(2):
                              last = e.tensor_scalar(out=PN[sl].ap[:, h * 256:(h + 1) * 256], in0=PX[sl].ap[:, h * 256:(h + 1) * 256],
                                                     scalar1=SS[sl].ap[:, h:h + 1], scalar2=None, op0=ALU.mult)
                          return last
                      cx.R.emit(DVE, f_pn, reads=[PX[sl], SS[sl]], writes=[PN[sl]])

                      def f_tr(e, sl=sl):
                          last = None
                          for q in range(4):
                              last = e.transpose(out=PT[sl].ap[:, q * 128:(q + 1) * 128], in_=PN[sl].ap[:, q * 128:(q + 1) * 128], identity=IDENT)
                          return last
                      cx.R.emit(PE, f_tr, reads=[PN[sl], CM], writes=[PT[sl]])
                      cx.R.emit(ACT, lambda e, sl=sl: e.activation(out=PTS[sl].ap[:], in_=PT[sl].ap[:, 0:512], func=AF.Copy),
                             reads=[PT[sl]], writes=[PTS[sl]])

                      def f_av(e, i=i, ip=ip, sl=sl):
                          last = None
                          for h in range(2):
                              for kh, kb in ((0, ip), (1, i)):
                                  last = e.matmul(OP[sl].ap[h * 64:(h + 1) * 64, 0:128], lhsT=VALL.ap[:, kb, 128:192],
                                                  rhs=PTS[sl].ap[:, (h * 2 + kh) * 128:(h * 2 + kh + 1) * 128],
                                                  start=(kh == 0), stop=(kh == 1))
                          return last
                      cx.R.emit(PE, f_av, reads=[VALL, PTS[sl]], writes=[OP[sl]])
                      g4 = i // 4
                      osl = g4 % 2
                      cx.R.emit(DVE, lambda e, sl=sl, osl=osl, i=i: e.tensor_copy(out=OAS[osl].ap[:, (i % 4) * 128:(i % 4 + 1) * 128], in_=OP[sl].ap[:, 0:128]),
                             reads=[OP[sl]], writes=[OAS[osl]])
                      if i % 4 == 3:
                          cx.R.emit(SP, lambda e, osl=osl, g4=g4: e.dma_start(out=gb_in.ap()[0:128, g4 * 512:(g4 + 1) * 512], in_=OAS[osl].ap[:]),
                                 reads=[OAS[osl]], writes=[GBIN_A[g4]], stream=sOAS[osl])
                  cx.R.barrier()

            with contextlib.ExitStack() as pc:
                ZB = [psum(pc, "ZB%d" % i, [128, 1024]) for i in range(2)]
                PCH = psum(pc, "PCH", [128, 1024])
                OB = [psum(pc, "OB%d" % i, [128, 512]) for i in range(2)]
                EB = [sbuf(pc, "EB%d" % i, [128, 1024], F32) for i in range(3)]
                LB = [sbuf(pc, "LB%d" % i, [128, 1024], BF16) for i in range(3)]
                XB = [sbuf(pc, "XB%d" % i, [128, 1024], BF16) for i in range(2)]
                AB = [sbuf(pc, "AB%d" % i, [128, 1024], BF16) for i in range(2)]
                OBS = [sbuf(pc, "OBS%d" % i, [128, 512], BF16) for i in range(2)]
                sOBS = [cx.R.new_stream() for _ in range(2)]
                steps = []
                for g in range(16 if DBG_1C_GROUPS is None else DBG_1C_GROUPS):
                    for j in range(4 * g + 3, -1, -1):
                        steps.append((g, j))
                NS = len(steps)

                def first(s):
                    return steps[s][1] == 4 * steps[s][0] + 3

                def last_(s):
                    return steps[s][1] == 0

                def emitZ(s):
                    g, j = steps[s]
                    Z = ZB[s % 2]

                    def f(e):
                        e.matmul(Z.ap[:, 0:512], lhsT=KBT[0].ap[:, j * 128:(j + 1) * 128], rhs=QBT.ap[:, g * 512:(g + 1) * 512],
                                 start=True, stop=True)
                        return e.matmul(Z.ap[:, 512:1024], lhsT=KBT[1].ap[:, j * 128:(j + 1) * 128],
                                        rhs=QBT.ap[:, g * 512:(g + 1) * 512], start=True, stop=True)
                    cx.R.emit(PE, f, reads=[KBT[0], KBT[1], QBT], writes=[Z])

                def emitEL(s):
                    g, j = steps[s]
                    Z = ZB[s % 2]
                    E = EB[s % 3]
                    L = LB[s % 3]
                    cx.R.emit(ACT, lambda e: e.activation(out=E.ap[:], in_=Z.ap[:], func=AF.Exp, scale=0.125), reads=[Z], writes=[E])
                    cx.R.emit(ACT, lambda e: e.activation(out=L.ap[:], in_=E.ap[:], func=AF.Ln, bias=1.0), reads=[E], writes=[L])
                    u = j - 4 * g
                    if u >= 0:
                        M = MASK.ap[:, u * 1024:(u + 1) * 1024]
                        cx.R.emit(DVE, lambda e: e.tensor_tensor(out=L.ap[:], in0=L.ap[:], in1=M, op=ALU.mult), reads=[L, MASK], writes=[L])
                        cx.R.emit(DVE, lambda e: e.tensor_tensor(out=E.ap[:], in0=E.ap[:], in1=M, op=ALU.mult), reads=[E, MASK], writes=[E])

                def emitTri(s, mat):
                    L = LB[s % 3]
                    st = first(s) and (mat is TRI)

                    def f(e):
                        e.matmul(PCH.ap[:, 0:512], lhsT=mat, rhs=L.ap[:, 0:512], start=st, stop=True, skip_group_check=True)
                        return e.matmul(PCH.ap[:, 512:1024], lhsT=mat, rhs=L.ap[:, 512:1024], start=st, stop=True, skip_group_check=True)
                    cx.R.emit(PE, f, reads=[L, CM], writes=[PCH])

                def emitExpP(s):
                    X = XB[s % 2]
                    cx.R.emit(ACT, lambda e: e.activation(out=X.ap[:], in_=PCH.ap[:], func=AF.Exp), reads=[PCH], writes=[X])

                def emitA(s):
                    X = XB[s % 2]
                    E = EB[s % 3]
                    A = AB[s % 2]
                    cx.R.emit(DVE, lambda e: e.tensor_tensor(out=A.ap[:], in0=E.ap[:], in1=X.ap[:], op=ALU.mult), reads=[E, X], writes=[A])

                def emitAV(s):
                    g, j = steps[s]
                    A = AB[s % 2]
                    O = OB[g % 2]
                    st, sp_ = first(s), last_(s)

                    def f(e):
                        e.matmul(O.ap[0:64, :], lhsT=VALL.ap[:, j, 0:64], rhs=A.ap[:, 0:512], start=st, stop=sp_, skip_group_check=True)
                        return e.matmul(O.ap[64:128, :], lhsT=VALL.ap[:, j, 64:128], rhs=A.ap[:, 512:1024], start=st, stop=sp_,
                                        skip_group_check=True)
                    cx.R.emit(PE, f, reads=[VALL, A], writes=[O])
                    if sp_:
                        osl = g % 2
                        cx.R.emit(DVE, lambda e: e.tensor_copy(out=OBS[osl].ap[:], in_=O.ap[:]), reads=[O], writes=[OBS[osl]])
                        cx.R.emit(SP, lambda e: e.dma_start(out=gb_in.ap()[128:256, g * 512:(g + 1) * 512], in_=OBS[osl].ap[:]),
                               reads=[OBS[osl]], writes=[GBIN_B[g]], stream=sOBS[osl])

                for s in range(-3, NS + 1):
                    if 0 <= s + 3 < NS:
                        emitZ(s + 3)
                    if 0 <= s < NS:
                        emitTri(s, TRI)
                        emitExpP(s)
                        emitA(s)
                    if 0 <= s + 2 < NS:
                        emitEL(s + 2)
                    if 0 <= s - 1 < NS:
                        emitAV(s - 1)
                    if 0 <= s < NS:
                        if not last_(s):
                            emitTri(s, TRI2)
                cx.R.barrier()

        if STOP_AFTER in ("a", "b", "c"):
            return nc
        xTo = din("xTo", [D, TOK], F32)
        wg = din("wg", [D, 2 * D], F32)
        wa = din("wa", [1024, D], F32)
        wb = din("wb", [1024, D], F32)
        wo = din("wo", [D, D], F32)
        wfi = din("wfi", [D, 2 * DFF], F32)
        wfd = din("wfd", [DFF, D], F32)
        xTo_v = xTo.rearrange("(k p) n -> p k n", p=128)
        GOUT = Buf("gb_out")
        sCC = cx.R.new_stream()
        cx.R.emit(POOL, lambda e: e.collective_compute("AllGather", ALU.bypass, replica_groups=[list(range(NCORE))],
                                                    ins=[gb_in.ap()], outs=[gb_out.ap()]),
               reads=GBIN_A + GBIN_B, writes=[GOUT], stream=sCC, inc=1)

        with contextlib.ExitStack() as p2:
            MG = sbuf(p2, "MG", [128, KC, TOK], BF16)
            PS = [psum(p2, "PS%d" % i, [128, 512]) for i in range(8)]
            ps_i = [0]

            def next_ps():
                b = PS[ps_i[0] % 8]
                ps_i[0] += 1
                return b

            TT = 256
            with contextlib.ExitStack() as pa:
                HN2 = sbuf(pa, "HN2", [128, KC, TOK], BF16)
                OT = [sbuf(pa, "OT%d" % k, [128, TOK], BF16) for k in range(16)]
                XS = [sbuf(pa, "XSb%d" % i, [128, KC, TT], F32) for i in range(1)]
                sXS = [cx.R.new_stream() for _ in range(1)]
                SQ = sbuf(pa, "SQb", [128, KC, TT], BF16)
                RB = [sbuf(pa, "RBb%d" % i, [128, TT], F32) for i in range(1)]
                WGA = [sbuf(pa, "WGA%d" % i, [128, KC, 256], BF16) for i in range(2)]
                WGB = [sbuf(pa, "WGB%d" % i, [128, KC, 256], BF16) for i in range(2)]
                WA = [sbuf(pa, "WA%d" % i, [128, 8, 256], BF16) for i in range(2)]
                WB = [sbuf(pa, "WB%d" % i, [128, 8, 256], BF16) for i in range(2)]
                sW = {n: [cx.R.new_stream() for _ in range(2)] for n in ("ga", "gb", "a", "b")}
                SGA = [sbuf(pa, "SGA%d" % i, [128, 512], BF16) for i in range(2)]
                SGB = [sbuf(pa, "SGB%d" % i, [128, 512], BF16) for i in range(2)]
                T1 = [sbuf(pa, "T1%d" % i, [128, 512], F32) for i in range(2)]
                T2 = [sbuf(pa, "T2%d" % i, [128, 512], F32) for i in range(2)]
                gview = gb_out.ap().rearrange("r (b n) -> (r b) n", b=NCORE)
                for k in range(16):
                    cx.R.emit(POOL, lambda e, k=k: e.indirect_dma_start(
                        out=OT[k].ap[:, :], out_offset=None, in_=gview,
                        in_offset=bass.IndirectOffsetOnAxis(ap=GIDX.ap[:, k:k + 1], axis=0)),
                        reads=[GOUT, GIDX], writes=[OT[k]], stream=cx.R.new_stream())
                for t in range(TOK // TT):
                    sl = 0
                    t0 = t * TT
                    load(SP, XS[sl], XS[sl].ap[:], xTo_v[:, :, t0:t0 + TT], sXS[sl])
                    norm_tile(XS[sl], XS[sl].ap[:], 0, SQ, next_ps(), RB[sl], HN2, lambda k, t0=t0: HN2.ap[:, k, t0:t0 + TT], TT)
                wg_v = wg.rearrange("(k p) n -> p k n", p=128)
                wa_v = wa.rearrange("(k p) n -> p k n", p=128)
                wb_v = wb.rearrange("(k p) n -> p k n", p=128)

                def load_mix_w(mg):
                    sl = mg % 2
                    c0 = mg * 256
                    load(POOL, WGA[sl], WGA[sl].ap[:], wg_v[:, :, c0:c0 + 256], sW["ga"][sl])
                    load(POOL, WGB[sl], WGB[sl].ap[:], wg_v[:, :, D + c0:D + c0 + 256], sW["gb"][sl])
                    load(POOL, WA[sl], WA[sl].ap[:], wa_v[:, :, c0:c0 + 256], sW["a"][sl])
                    load(POOL, WB[sl], WB[sl].ap[:], wb_v[:, :, c0:c0 + 256], sW["b"][sl])

                load_mix_w(0)
                it = 0
                for mg in range(8):
                    if mg + 1 < 8:
                        load_mix_w(mg + 1)
                    sl = mg % 2
                    for mi in range(2):
                        m = mg * 2 + mi
                        cs = slice(mi * 128, (mi + 1) * 128)
                        for half in range(2):
                            hs = slice(half * 512, (half + 1) * 512)
                            ts = it % 2
                            it += 1
                            Pga, Pgb, Pya, Pyb = next_ps(), next_ps(), next_ps(), next_ps()
                            cx.R.emit(PE, lambda e, P=Pga, sl=sl, cs=cs, hs=hs: mm_group(
                                e, P.ap[:], [(WGA[sl].ap[:, k, cs], HN2.ap[:, k, hs]) for k in range(KC)]), reads=[WGA[sl], HN2], writes=[Pga])
                            cx.R.emit(ACT, lambda e, P=Pga, ts=ts: e.activation(out=SGA[ts].ap[:], in_=P.ap[:], func=AF.Sigmoid), reads=[Pga], writes=[SGA[ts]])
                            cx.R.emit(PE, lambda e, P=Pgb, sl=sl, cs=cs, hs=hs: mm_group(
                                e, P.ap[:], [(WGB[sl].ap[:, k, cs], HN2.ap[:, k, hs]) for k in range(KC)]), reads=[WGB[sl], HN2], writes=[Pgb])
                            cx.R.emit(ACT, lambda e, P=Pgb, ts=ts: e.activation(out=SGB[ts].ap[:], in_=P.ap[:], func=AF.Sigmoid), reads=[Pgb], writes=[SGB[ts]])
                            cx.R.emit(PE, lambda e, P=Pya, sl=sl, cs=cs, hs=hs: mm_group(
                                e, P.ap[:], [(WA[sl].ap[:, k, cs], OT[k].ap[:, hs]) for k in range(8)]), reads=[WA[sl]] + OT[0:8], writes=[Pya])
                            cx.R.emit(PE, lambda e, P=Pyb, sl=sl, cs=cs, hs=hs: mm_group(
                                e, P.ap[:], [(WB[sl].ap[:, k, cs], OT[8 + k].ap[:, hs]) for k in range(8)]), reads=[WB[sl]] + OT[8:16], writes=[Pyb])
                            cx.R.emit(DVE, lambda e, P=Pya, ts=ts: e.tensor_tensor(out=T1[ts].ap[:], in0=P.ap[:], in1=SGA[ts].ap[:], op=ALU.mult),
                                   reads=[Pya, SGA[ts]], writes=[T1[ts]])
                            cx.R.emit(DVE, lambda e, P=Pyb, ts=ts: e.tensor_tensor(out=T2[ts].ap[:], in0=P.ap[:], in1=SGB[ts].ap[:], op=ALU.mult),
                                   reads=[Pyb, SGB[ts]], writes=[T2[ts]])
                            cx.R.emit(DVE, lambda e, ts=ts, m=m, hs=hs: e.tensor_tensor(out=MG.ap[:, m, hs], in0=T1[ts].ap[:], in1=T2[ts].ap[:], op=ALU.add),
                                   reads=[T1[ts], T2[ts]], writes=[MG])
                cx.R.barrier()

            with contextlib.ExitStack() as pb:
                XS2 = sbuf(pb, "XS2", [128, KC, TOK], F32)
                HN3 = MG
                SQ = sbuf(pb, "SQc", [128, KC, TT], BF16)
                RB = [sbuf(pb, "RBc%d" % i, [128, TT], F32) for i in range(2)]
                WO = [sbuf(pb, "WO%d" % i, [128, KC, 256], BF16) for i in range(2)]
                sWO = [cx.R.new_stream() for _ in range(2)]
                WGT = [sbuf(pb, "WGT%d" % i, [128, KC, 256], BF16) for i in range(2)]
                WUP = [sbuf(pb, "WUP%d" % i, [128, KC, 256], BF16) for i in range(2)]
                WDN = [sbuf(pb, "WDN%d" % i, [128, 2, D], BF16) for i in range(2)]
                sWF = {n: [cx.R.new_stream() for _ in range(2)] for n in ("g", "u", "d")}
                SGT = [sbuf(pb, "SGT%d" % i, [128, 512], F32) for i in range(2)]
                AT = [sbuf(pb, "AT%d" % i, [128, 2, TOK], BF16) for i in range(2)]
                sX2 = cx.R.new_stream()
                for q in range(4):
                    load(SP, XS2, XS2.ap[:, 4 * q:4 * q + 4, :], xTo_v[:, 4 * q:4 * q + 4, :], sX2)
                wo_v = wo.rearrange("(k p) n -> p k n", p=128)
                wfi_v = wfi.rearrange("(k p) n -> p k n", p=128)
                wfd_v = wfd.rearrange("(j p) n -> p j n", p=128)

                def load_wo(mg):
                    sl = mg % 2
                    load(POOL, WO[sl], WO[sl].ap[:], wo_v[:, :, mg * 256:(mg + 1) * 256], sWO[sl])

                def load_ffn(gi):
                    sl = gi % 2
                    c0 = gi * 256
                    load(POOL, WGT[sl], WGT[sl].ap[:], wfi_v[:, :, c0:c0 + 256], sWF["g"][sl])
                    load(POOL, WUP[sl], WUP[sl].ap[:], wfi_v[:, :, DFF + c0:DFF + c0 + 256], sWF["u"][sl])
                    load(POOL, WDN[sl], WDN[sl].ap[:], wfd_v[:, 2 * gi:2 * gi + 2, :], sWF["d"][sl])

                load_wo(0)
                for mg in range(8):
                    if mg + 1 < 8:
                        load_wo(mg + 1)
                    else:
                        load_ffn(0)
                    sl = mg % 2
                    for mi in range(2):
                        m = mg * 2 + mi
                        cs = slice(mi * 128, (mi + 1) * 128)
                        for half in range(2):
                            hs = slice(half * 512, (half + 1) * 512)
                            P = next_ps()
                            cx.R.emit(PE, lambda e, P=P, sl=sl, cs=cs, hs=hs: mm_group(
                                e, P.ap[:], [(WO[sl].ap[:, k, cs], MG.ap[:, k, hs]) for k in range(KC)]), reads=[WO[sl], MG], writes=[P])
                            cx.R.emit(DVE, lambda e, P=P, m=m, hs=hs: e.tensor_tensor(out=XS2.ap[:, m, hs], in0=P.ap[:], in1=XS2.ap[:, m, hs], op=ALU.add),
                                   reads=[P, XS2], writes=[XS2])
                for t in range(TOK // TT):
                    t0 = t * TT
                    norm_tile(XS2, XS2.ap[:, :, t0:t0 + TT], KC, SQ, next_ps(), RB[t % 2], HN3, lambda k, t0=t0: HN3.ap[:, k, t0:t0 + TT], TT)
                NG = DFF // 256
                for gi in range(NG):
                    if gi + 1 < NG:
                        load_ffn(gi + 1)
                    sl = gi % 2
                    for jj in range(2):
                        cs = slice(jj * 128, (jj + 1) * 128)
                        for half in range(2):
                            hs = slice(half * 512, (half + 1) * 512)
                            Pg, Pu = next_ps(), next_ps()
                            ts = (jj * 2 + half) % 2
                            cx.R.emit(PE, lambda e, P=Pg, sl=sl, cs=cs, hs=hs: mm_group(
                                e, P.ap[:], [(WGT[sl].ap[:, k, cs], HN3.ap[:, k, hs]) for k in range(KC)]), reads=[WGT[sl], HN3], writes=[Pg])
                            cx.R.emit(ACT, lambda e, P=Pg, ts=ts: e.activation(out=SGT[ts].ap[:], in_=P.ap[:], func=AF.Silu), reads=[Pg], writes=[SGT[ts]])
                            cx.R.emit(PE, lambda e, P=Pu, sl=sl, cs=cs, hs=hs: mm_group(
                                e, P.ap[:], [(WUP[sl].ap[:, k, cs], HN3.ap[:, k, hs]) for k in range(KC)]), reads=[WUP[sl], HN3], writes=[Pu])
                            cx.R.emit(DVE, lambda e, P=Pu, ts=ts, sl=sl, jj=jj, hs=hs: e.tensor_tensor(
                                out=AT[sl].ap[:, jj, hs], in0=P.ap[:], in1=SGT[ts].ap[:], op=ALU.mult), reads=[Pu, SGT[ts]], writes=[AT[sl]])
                    for m in range(KC):
                        for half in range(2):
                            hs = slice(half * 512, (half + 1) * 512)
                            P = next_ps()
                            cx.R.emit(PE, lambda e, P=P, sl=sl, m=m, hs=hs: mm_group(
                                e, P.ap[:], [(WDN[sl].ap[:, jj, m * 128:(m + 1) * 128], AT[sl].ap[:, jj, hs]) for jj in range(2)]),
                                reads=[WDN[sl], AT[sl]], writes=[P])
                            cx.R.emit(DVE, lambda e, P=P, m=m, hs=hs: e.tensor_tensor(out=XS2.ap[:, m, hs], in0=P.ap[:], in1=XS2.ap[:, m, hs], op=ALU.add),
                                   reads=[P, XS2], writes=[XS2])
                OUTB = Buf("outT")
                sOUT = cx.R.new_stream()
                outT_v = outT.rearrange("(k p) n -> p k n", p=128)
                for t in range(TOK // TT):
                    t0 = t * TT
                    norm_tile(XS2, XS2.ap[:, :, t0:t0 + TT], 2 * KC, SQ, next_ps(), RB[t % 2], XS2, lambda k, t0=t0: XS2.ap[:, k, t0:t0 + TT], TT)
                for q in range(4):
                    cx.R.emit(SP, lambda e, q=q: e.dma_start(out=outT_v[:, 4 * q:4 * q + 4, :], in_=XS2.ap[:, 4 * q:4 * q + 4, :]),
                              reads=[XS2], writes=[OUTB], stream=sOUT)
                cx.R.replay(nc, top, "e")
    return nc


def _host_constants():
    bf = ml_dtypes.bfloat16
    ones = np.ones((128, 128), np.float32)
    ident = np.eye(128, dtype=np.float32)
    sp = np.arange(128)[:, None]
    s_ = np.arange(128)[None, :]
    tri = np.where(sp >= s_, -1.0, 0.0).astype(np.float32)
    tri2 = np.where(sp < s_, -1.0, 0.0).astype(np.float32)
    cmat = np.concatenate([ones, ident, tri, tri2], axis=1).astype(bf)
    masks = []
    ql = np.arange(512)[None, :]
    r = ql // 128
    tl = ql % 128
    sl = np.arange(128)[:, None]
    for u in range(4):
        m = ((u < r) | ((u == r) & (sl < tl))).astype(np.float32)
        masks.append(np.concatenate([m, m], axis=1))
    masks = np.concatenate(masks, axis=1).astype(bf)
    return cmat, masks


def kernel(x, norm_mix_g, w_in, sink_logits, w_branch_a, w_branch_b, w_out,
           norm_ffn_g, w_ffn_in, w_ffn_down, norm_final_g):
    f32 = np.float32
    x = np.asarray(x, f32)
    W = np.asarray(w_in, f32)[0]
    xT = np.ascontiguousarray(x[0].T)
    wg = np.ascontiguousarray(W[:, 4608:8704])
    wa = np.ascontiguousarray(np.asarray(w_branch_a, f32)[0])
    wb = np.ascontiguousarray(np.asarray(w_branch_b, f32)[0])
    wo = np.ascontiguousarray(np.asarray(w_out, f32)[0])
    wfi = np.ascontiguousarray(np.asarray(w_ffn_in, f32)[0])
    wfd = np.ascontiguousarray(np.asarray(w_ffn_down, f32)[0])
    gs = [np.asarray(norm_mix_g, f32)[0], np.asarray(norm_ffn_g, f32)[0], np.asarray(norm_final_g, f32)]
    gvec = np.ascontiguousarray(np.concatenate([g.reshape(KC, 128).T for g in gs], axis=1))
    sink = np.asarray(sink_logits, f32)[0]
    cmat, masks = _host_constants()
    qi = np.arange(128)[:, None]
    ki = np.arange(256)[None, :]
    dist = (128 + qi - ki).astype(f32)
    valid = (dist >= 0) & (dist < 128)

    in_maps = []
    for c in range(NCORE):
        kv = c // 2
        cols = np.concatenate([
            np.arange(1536 + 128 * c, 1536 + 128 * c + 128),
            np.arange(2560 + 128 * c, 2560 + 128 * c + 128),
            np.arange(128 * c, 128 * c + 128),
            np.arange(1024 + 64 * kv, 1024 + 64 * kv + 64),
            np.arange(1024 + 64 * kv, 1024 + 64 * kv + 64),
            np.arange(3584 + 128 * c, 3584 + 128 * c + 128),
            np.arange(1280 + 64 * kv, 1280 + 64 * kv + 64),
        ])
        w1 = np.ascontiguousarray(W[:, cols])
        swab = np.empty((128, 1024), f32)
        for h in range(2):
            hh = 2 * c + h
            slope = f32(2.0) ** f32(-8.0 * (hh + 1) / 16.0)
            b = np.where(valid, -slope * dist, f32(NEG_BIG)).astype(f32)
            b0 = b.copy()
            b0[:, 0:128] = NEG_BIG
            if SWA_T:
                b = np.concatenate([b[:, 0:128].T, b[:, 128:256].T], axis=1)
                b0 = np.concatenate([b0[:, 0:128].T, b0[:, 128:256].T], axis=1)
            swab[:, h * 256:(h + 1) * 256] = b
            swab[:, 512 + h * 256:512 + (h + 1) * 256] = b0
        sinkt = np.ascontiguousarray(np.broadcast_to(sink[2 * c:2 * c + 2][None, :], (128, 2))).astype(f32)
        p = np.arange(128)[:, None]
        cc = np.arange(8)[None, :]
        gidx = np.concatenate([(cc * 256 + p) * 8 + c, (cc * 256 + 128 + p) * 8 + c], axis=1).astype(np.int32)
        in_maps.append({
            "xT": xT, "xTo": np.ascontiguousarray(xT[:, c * TOK:(c + 1) * TOK]), "w1": w1, "wg": wg, "wa": wa, "wb": wb,
            "wo": wo, "wfi": wfi, "wfd": wfd, "gvec": gvec, "swab": swab, "sinkt": sinkt,
            "gidx": np.ascontiguousarray(gidx), "cmat": cmat, "masks": masks, "identf": np.eye(128, dtype=np.float32),
        })
    nc = build_program()
    res = run_bass_kernel_spmd(nc, in_maps, core_ids=list(range(NCORE)))
    out = np.empty((1, S, D), f32)
    for c in range(NCORE):
        out[0, c * TOK:(c + 1) * TOK, :] = np.asarray(res.results[c]["outT"], f32).T
    return out
```
